# Optimizing a Trainium2 kernel written in Bass

```python
import jax
import jax.numpy as jnp
from jax import lax
import numpy as np

D_MODEL = 2048
BATCH = 1
SEQ = 8192
DEPTH = 1
DEC_BATCH = 8
DEC_SEQ = 4096
PAST_LEN = 128

GRID_W = 64
NA_HEADS = 8
NA_HEAD_DIM = 128
NA_WIDTH = NA_HEADS * NA_HEAD_DIM
NA_KH_MAX = 8
NA_KW = 16
NA_QB = 16
NA_KB = 32
NA_NCB = GRID_W // NA_QB
ML_HEADS = 4
ML_HEAD_DIM = 256
ML_WIDTH = ML_HEADS * ML_HEAD_DIM
ML_CHUNK = 128
ML_CONV = 5
D_FF = 5632
N_GATE = 4 * ML_HEADS
SPLIT_SIZES = (NA_WIDTH, NA_WIDTH, NA_WIDTH, 2 * ML_WIDTH, ML_WIDTH, ML_WIDTH, N_GATE, D_MODEL, D_MODEL)
D_IN = sum(SPLIT_SIZES)
ALPHA = (2 * DEPTH) ** 0.25
BETA = (8 * DEPTH) ** -0.25
LN_EPS = 1e-5

kernel_name = "hybrid_natten_mlstm_macaron_deepnorm_encoder"


def _split_points():
    return np.cumsum(np.array(SPLIT_SIZES))[:-1].tolist()


def _layernorm(x, g, b):
    xf = x.astype(jnp.float32)
    mu = xf.mean(-1, keepdims=True)
    var = jnp.square(xf - mu).mean(-1, keepdims=True)
    return ((xf - mu) * lax.rsqrt(var + LN_EPS) * g + b).astype(x.dtype)


def _swiglu(x, w_gu, w_down):
    g, u = jnp.split(x @ w_gu, 2, axis=-1)
    return (jax.nn.silu(g) * u) @ w_down


def _na_col_tables():
    qc = np.arange(GRID_W).reshape(NA_NCB, NA_QB)
    kstart = np.clip(np.arange(NA_NCB) * NA_QB - NA_KW // 2, 0, GRID_W - NA_KB)
    kc = kstart[:, None] + np.arange(NA_KB)[None, :]
    ws = np.clip(qc - NA_KW // 2, 0, GRID_W - NA_KW)
    kcb = kc[:, None, :]
    valid = (kcb >= ws[:, :, None]) & (kcb < ws[:, :, None] + NA_KW)
    col_idx = np.clip(kcb - qc[:, :, None] + NA_KW - 1, 0, 2 * NA_KW - 2)
    return kc, col_idx, valid


def _neighbourhood_attention(q, k, v, rpb):
    B, N, _ = q.shape
    rows = N // GRID_W
    kh = min(NA_KH_MAX, rows)

    def to_grid(t):
        return t.reshape(B, rows, GRID_W, NA_HEADS, NA_HEAD_DIM).transpose(0, 3, 1, 2, 4)

    qg = to_grid(q) * (NA_HEAD_DIM ** -0.5)
    kg = to_grid(k)
    vg = to_grid(v)
    kc, col_idx, valid = _na_col_tables()
    kc_flat = jnp.asarray(kc.reshape(-1))
    bias_col = rpb[:, :, col_idx].astype(jnp.float32)
    valid_m = jnp.asarray(valid)[None, None, :, :, None, :]
    q_rows = qg.reshape(B, NA_HEADS, rows, NA_NCB, NA_QB, NA_HEAD_DIM).transpose(2, 0, 1, 3, 4, 5)

    def one_row(args):
        r, qr = args
        rs = jnp.clip(r - kh // 2, 0, rows - kh)
        kb = jnp.take(lax.dynamic_slice_in_dim(kg, rs, kh, axis=2), kc_flat, axis=3)
        vb = jnp.take(lax.dynamic_slice_in_dim(vg, rs, kh, axis=2), kc_flat, axis=3)
        kb = kb.reshape(B, NA_HEADS, kh, NA_NCB, NA_KB, NA_HEAD_DIM)
        vb = vb.reshape(B, NA_HEADS, kh, NA_NCB, NA_KB, NA_HEAD_DIM)
        s = jnp.einsum('bhcqd,bhackd->bhcqak', qr, kb).astype(jnp.float32)
        row_idx = rs + jnp.arange(kh) - r + (NA_KH_MAX - 1)
        bias = jnp.take(bias_col, row_idx, axis=1).transpose(0, 2, 3, 1, 4)
        s = jnp.where(valid_m, s + bias[None], -jnp.inf)
        p = jax.nn.softmax(s.reshape(B, NA_HEADS, NA_NCB, NA_QB, kh * NA_KB), axis=-1)
        p = p.reshape(B, NA_HEADS, NA_NCB, NA_QB, kh, NA_KB).astype(vb.dtype)
        return jnp.einsum('bhcqak,bhackd->bhcqd', p, vb)

    out = lax.map(one_row, (jnp.arange(rows), q_rows))
    return out.transpose(1, 0, 3, 4, 2, 5).reshape(B, N, NA_WIDTH)


def _to_chunks(t, nc):
    G, H = t.shape[:2]
    return jnp.moveaxis(t.reshape((G, H, nc, ML_CHUNK) + t.shape[3:]), 2, 0)


def _mlstm_chunkwise(q, k, v, log_i, log_f):
    G, H, N, d = q.shape
    nc = N // ML_CHUNK
    causal = jnp.tril(jnp.ones((ML_CHUNK, ML_CHUNK), dtype=bool))

    def step(carry, inp):
        C, n, m = carry
        qb, kb, vb, ib, fb = inp
        b = jnp.cumsum(fb, axis=-1)
        dmat = jnp.where(causal, b[..., :, None] - b[..., None, :] + ib[..., None, :], -jnp.inf)
        inter = b + m[..., None]
        mt = jnp.maximum(inter, dmat.max(-1))
        sc = jnp.einsum('ghtd,ghsd->ghts', qb, kb) * jnp.exp(dmat - mt[..., None])
        a_inter = jnp.exp(inter - mt)
        num = jnp.einsum('ghts,ghsd->ghtd', sc, vb) + a_inter[..., None] * jnp.einsum('ghtd,ghde->ghte', qb, C)
        den = sc.sum(-1) + a_inter * jnp.einsum('ghtd,ghd->ght', qb, n)
        h = num / jnp.maximum(jnp.abs(den), jnp.exp(-mt))[..., None]
        b_last = b[..., -1]
        g = b_last[..., None] - b + ib
        m_new = jnp.maximum(b_last + m, g.max(-1))
        wk = kb * jnp.exp(g - m_new[..., None])[..., None]
        decay = jnp.exp(b_last + m - m_new)
        C_new = decay[..., None, None] * C + jnp.einsum('ghsd,ghse->ghde', wk, vb)
        n_new = decay[..., None] * n + wk.sum(axis=2)
        return (C_new, n_new, m_new), h

    init = (jnp.zeros((G, H, d, d), jnp.float32), jnp.zeros((G, H, d), jnp.float32), jnp.zeros((G, H), jnp.float32))
    xs = (_to_chunks(q, nc), _to_chunks(k, nc), _to_chunks(v, nc), _to_chunks(log_i, nc), _to_chunks(log_f, nc))
    _, hs = lax.scan(step, init, xs)
    return jnp.moveaxis(hs, 0, 2).reshape(G, H, N, d)


def _mlstm_branch(qk_raw, v, o, gates, conv_w, conv_b, gate_b, norm_g):
    B, N, _ = v.shape
    pad = ML_CONV // 2
    qk = lax.conv_general_dilated(qk_raw, conv_w[:, None, :], window_strides=(1,), padding=[(pad, pad)],
                                  dimension_numbers=('NWC', 'WIO', 'NWC'),
                                  feature_group_count=2 * ML_WIDTH) + conv_b
    qk = jax.nn.silu(qk).astype(jnp.float32)
    q, k = jnp.split(qk, 2, axis=-1)

    def heads(t):
        return t.reshape(B, N, ML_HEADS, ML_HEAD_DIM).transpose(0, 2, 1, 3)

    q = heads(q)
    k = heads(k) * (ML_HEAD_DIM ** -0.5)
    vh = heads(v.astype(jnp.float32))
    gt = (gates.astype(jnp.float32) + gate_b.astype(jnp.float32)).transpose(0, 2, 1)
    i_f, i_b, f_f, f_b = jnp.split(gt, 4, axis=1)
    flip = lambda t: jnp.flip(t, axis=2)
    qq = jnp.concatenate([q, flip(q)], axis=0)
    kk = jnp.concatenate([k, flip(k)], axis=0)
    vv = jnp.concatenate([vh, flip(vh)], axis=0)
    li = jnp.concatenate([i_f, flip(i_b)], axis=0)
    lf = jax.nn.log_sigmoid(jnp.concatenate([f_f, flip(f_b)], axis=0))
    h = _mlstm_chunkwise(qq, kk, vv, li, lf)
    h = h[:B] + flip(h[B:])
    mu = h.mean(-1, keepdims=True)
    var = jnp.square(h - mu).mean(-1, keepdims=True)
    h = (h - mu) * lax.rsqrt(var + LN_EPS) * norm_g.astype(jnp.float32).reshape(ML_HEADS, ML_HEAD_DIM)[None, :, None, :]
    h = h.transpose(0, 2, 1, 3).reshape(B, N, ML_WIDTH)
    return (h * jax.nn.sigmoid(o.astype(jnp.float32))).astype(v.dtype)


def _mixer(x, w_in, rpb, conv_w, conv_b, gate_b, ml_norm_g, w_pa, w_pm, w_out):
    z = x @ w_in
    q_a, k_a, v_a, qk_m, v_m, o_m, gates, g_a, g_m = jnp.split(z, _split_points(), axis=-1)
    y_a = _neighbourhood_attention(q_a, k_a, v_a, rpb)
    y_m = _mlstm_branch(qk_m, v_m, o_m, gates, conv_w, conv_b, gate_b, ml_norm_g)
    merged = jax.nn.sigmoid(g_a) * (y_a @ w_pa) + jax.nn.sigmoid(g_m) * (y_m @ w_pm)
    return merged @ w_out


def _encoder(x, params):
    (ffa_w_gu, ffa_w_down, norm_a_g, norm_a_b, mix_w_in, na_rpb, ml_conv_w, ml_conv_b, ml_gate_b,
     ml_norm_g, mix_w_pa, mix_w_pm, mix_w_out, norm_m_g, norm_m_b, ffb_w_gu, ffb_w_down,
     norm_b_g, norm_b_b) = params
    for l in range(DEPTH):
        x = _layernorm(ALPHA * x + 0.5 * _swiglu(x, ffa_w_gu[l], ffa_w_down[l]), norm_a_g[l], norm_a_b[l])
        mix = _mixer(x, mix_w_in[l], na_rpb[l], ml_conv_w[l], ml_conv_b[l], ml_gate_b[l], ml_norm_g[l],
                     mix_w_pa[l], mix_w_pm[l], mix_w_out[l])
        x = _layernorm(ALPHA * x + mix, norm_m_g[l], norm_m_b[l])
        x = _layernorm(ALPHA * x + 0.5 * _swiglu(x, ffb_w_gu[l], ffb_w_down[l]), norm_b_g[l], norm_b_b[l])
    return x


def setup_inputs(seed: int = 0) -> dict:
    key = jax.random.key(seed)
    ks = jax.random.split(key, 24)
    f32 = jnp.float32

    def nrm(k, shape, scale):
        return jax.random.normal(k, shape, f32) * scale

    i_bias = nrm(ks[10], (DEPTH, 2 * ML_HEADS), 0.1)
    f_bias = jnp.tile(jnp.linspace(3.0, 6.0, ML_HEADS, dtype=f32), (DEPTH, 2)) + nrm(ks[11], (DEPTH, 2 * ML_HEADS), 0.01)
    return {
        "x_prompt": nrm(ks[0], (BATCH, SEQ, D_MODEL), 1.0),
        "x_sample": nrm(ks[1], (DEC_BATCH, DEC_SEQ, D_MODEL), 1.0),
        "ffa_w_gu": nrm(ks[2], (DEPTH, D_MODEL, 2 * D_FF), D_MODEL ** -0.5),
        "ffa_w_down": nrm(ks[3], (DEPTH, D_FF, D_MODEL), BETA * D_FF ** -0.5),
        "norm_a_g": 1.0 + nrm(ks[4], (DEPTH, D_MODEL), 0.02),
        "norm_a_b": nrm(ks[5], (DEPTH, D_MODEL), 0.02),
        "mix_w_in": nrm(ks[6], (DEPTH, D_MODEL, D_IN), D_MODEL ** -0.5),
        "na_rpb": nrm(ks[7], (DEPTH, NA_HEADS, 2 * NA_KH_MAX - 1, 2 * NA_KW - 1), 0.02),
        "ml_conv_w": nrm(ks[8], (DEPTH, ML_CONV, 2 * ML_WIDTH), ML_CONV ** -0.5),
        "ml_conv_b": nrm(ks[9], (DEPTH, 2 * ML_WIDTH), 0.02),
        "ml_gate_b": jnp.concatenate([i_bias, f_bias], axis=-1),
        "ml_norm_g": 1.0 + nrm(ks[12], (DEPTH, ML_WIDTH), 0.02),
        "mix_w_pa": nrm(ks[13], (DEPTH, NA_WIDTH, D_MODEL), NA_WIDTH ** -0.5),
        "mix_w_pm": nrm(ks[14], (DEPTH, ML_WIDTH, D_MODEL), ML_WIDTH ** -0.5),
        "mix_w_out": nrm(ks[15], (DEPTH, D_MODEL, D_MODEL), BETA * D_MODEL ** -0.5),
        "norm_m_g": 1.0 + nrm(ks[16], (DEPTH, D_MODEL), 0.02),
        "norm_m_b": nrm(ks[17], (DEPTH, D_MODEL), 0.02),
        "ffb_w_gu": nrm(ks[18], (DEPTH, D_MODEL, 2 * D_FF), D_MODEL ** -0.5),
        "ffb_w_down": nrm(ks[19], (DEPTH, D_FF, D_MODEL), BETA * D_FF ** -0.5),
        "norm_b_g": 1.0 + nrm(ks[20], (DEPTH, D_MODEL), 0.02),
        "norm_b_b": nrm(ks[21], (DEPTH, D_MODEL), 0.02),
    }


def reference(x_prompt, x_sample, ffa_w_gu, ffa_w_down, norm_a_g, norm_a_b, mix_w_in, na_rpb,
              ml_conv_w, ml_conv_b, ml_gate_b, ml_norm_g, mix_w_pa, mix_w_pm, mix_w_out,
              norm_m_g, norm_m_b, ffb_w_gu, ffb_w_down, norm_b_g, norm_b_b):
    params = (ffa_w_gu, ffa_w_down, norm_a_g, norm_a_b, mix_w_in, na_rpb, ml_conv_w, ml_conv_b,
              ml_gate_b, ml_norm_g, mix_w_pa, mix_w_pm, mix_w_out, norm_m_g, norm_m_b,
              ffb_w_gu, ffb_w_down, norm_b_g, norm_b_b)
    y_prompt = _encoder(x_prompt, params)
    y_sample = _encoder(x_sample, params)
    return (y_prompt, y_sample)
```

```python
import contextlib
import math
import numpy as np
import concourse.bass as bass
import concourse.mybir as mybir
from concourse.bass_utils import run_bass_kernel_spmd

F32 = mybir.dt.float32
BF16 = mybir.dt.bfloat16
AF = mybir.ActivationFunctionType
ALU = mybir.AluOpType
AX = mybir.AxisListType

D = 2048
KC = 16
DFF = 5632
JC = 44
GW = 64
ALPHA = 2.0 ** 0.25
LN_EPS = 1e-5
TT = 512
NEG = -30000.0
N_CORES = 8
NT_FULL = 8192


BISECT = None


class _Stop(Exception):
    pass


def _chk(name):
    if BISECT == name:
        raise _Stop()


class Buf:
    __slots__ = ("w", "r", "x")

    def __init__(self, w=None, x=False):
        self.w = w
        self.r = {}
        self.x = x


class Prog:
    ENG = ("pe", "act", "dve", "pool", "sp")
    LIM = 30000
    NSLOT = 40

    def __init__(self, nc, es):
        self.nc = nc
        self.es = es
        self.ops = {e: [] for e in self.ENG}
        self.known = {e: {} for e in self.ENG}
        self.slot_count = [0] * self.NSLOT
        self.slot_rng = {"sp": (0, 24), "pool": (24, 40), "act": (24, 40)}
        self.next_slot = {"sp": 0, "pool": 24}
        self.phase_w = None
        self.allbufs = []

    def buf(self):
        b = Buf(self.phase_w)
        self.allbufs.append(b)
        return b

    def bufs(self, n):
        return [self.buf() for _ in range(n)]

    def pbuf(self):
        b = Buf(self.phase_w, True)
        self.allbufs.append(b)
        return b

    def pbufs(self, n):
        return [self.pbuf() for _ in range(n)]

    def sb(self, name, shape, dt, es=None):
        self.ntens = getattr(self, "ntens", 0) + 1
        return (es or self.es).enter_context(self.nc.sbuf_tensor(f"s{self.ntens}_{name}", list(shape), dt))

    def ps(self, name, shape, dt, es=None):
        self.ntens = getattr(self, "ntens", 0) + 1
        return (es or self.es).enter_context(self.nc.psum_tensor(f"p{self.ntens}_{name}", list(shape), dt))

    def _record(self, e, fn, reads, writes, dma_slot=None):
        xr = [b for b in reads if b.x]
        if xr:
            writes = list(writes) + xr
            reads = [b for b in reads if not b.x]
        d = {}
        for b in reads:
            if b.w is not None:
                k, i = b.w
                if d.get(k, -1) < i:
                    d[k] = i
        for b in writes:
            if b.w is not None:
                k, i = b.w
                if d.get(k, -1) < i:
                    d[k] = i
            for k, i in b.r.items():
                if d.get(k, -1) < i:
                    d[k] = i
        if dma_slot is not None:
            k = ("dma", dma_slot)
            pc = self.slot_count[dma_slot]
            if pc > 0 and d.get(k, -1) < pc:
                d[k] = pc
        waits = []
        kn = self.known[e]
        for k, i in d.items():
            if k == "pe" and e == "pe":
                continue
            if k == e and len(self.ops[e]) - i > 3:
                continue
            if kn.get(k, -1) >= i:
                continue
            kn[k] = i
            waits.append((k, i))
            if not isinstance(k, tuple):
                self.ops[k][i][2] = True
        idx = len(self.ops[e])
        self.ops[e].append([fn, waits, False, dma_slot, 0])
        if dma_slot is not None:
            self.slot_count[dma_slot] += 16
            key = ("dma", dma_slot)
            val = self.slot_count[dma_slot]
        else:
            key = e
            val = idx
        for b in reads:
            if b.r.get(key, -1) < val:
                b.r[key] = val
        for b in writes:
            b.w = (key, val)
            b.r = {}
        return idx

    def op(self, e, meth, *args, reads=(), writes=(), **kw):
        def fn(eng):
            return getattr(eng, meth)(*args, **kw)
        return self._record(e, fn, reads, writes)

    def dma(self, q, out, in_, reads=(), writes=(), **kw):
        lo, hi = self.slot_rng[q]
        slot = self.next_slot[q]
        self.next_slot[q] = lo + (slot + 1 - lo) % (hi - lo)

        def fn(eng):
            return eng.dma_start(out=out, in_=in_, **kw)
        return self._record(q, fn, reads, writes, dma_slot=slot)

    def barrier(self):
        def fn(eng):
            return eng.nop()
        idx = self._record("sp", fn, [], self.allbufs)
        self.ops["sp"][idx][2] = True
        self.phase_w = ("sp", idx)

    def final_wait(self, bufs):
        def fn(eng):
            return eng.nop()
        return self._record("sp", fn, list(bufs), [])

    def emit(self):
        nc = self.nc
        es = self.es
        nsem = {}
        for e in self.ENG:
            c = 0
            for o in self.ops[e]:
                if o[2]:
                    c += 1
                    o[4] = c
            nsem[e] = (c + self.LIM - 1) // self.LIM
        sems = {e: [es.enter_context(nc.semaphore(f"s_{e}{j}")) for j in range(nsem[e])] for e in self.ENG}
        dsem = {s: es.enter_context(nc.semaphore(f"d_{s}")) for s in range(self.NSLOT) if self.slot_count[s] > 0}
        ops = self.ops
        LIM = self.LIM

        def run(e, eng):
            for fn, waits, inc, slot, cnt in ops[e]:
                for k, i in waits:
                    if isinstance(k, tuple):
                        eng.wait_ge(dsem[k[1]], i)
                    else:
                        c = ops[k][i][4]
                        eng.wait_ge(sems[k][(c - 1) // LIM], (c - 1) % LIM + 1)
                ins = fn(eng)
                if slot is not None:
                    ins.then_inc(dsem[slot], 16)
                elif inc:
                    ins.then_inc(sems[e][(cnt - 1) // LIM], 1)

        with nc.Block() as block:
            @block.tensor
            def _(eng):
                run("pe", eng)

            @block.scalar
            def _(eng):
                run("act", eng)

            @block.vector
            def _(eng):
                run("dve", eng)

            @block.gpsimd
            def _(eng):
                run("pool", eng)

            @block.sync
            def _(eng):
                run("sp", eng)


class WStream:
    def __init__(self, P, ring, rbufs, units, depth=2):
        self.P = P
        self.ring = ring
        self.rbufs = rbufs
        self.units = units
        self.issued = 0
        self.base = 0
        self.depth = depth

    def _issue(self, i):
        P = self.P
        ap, n, cols, srcb = self.units[i]
        s = i % len(self.ring)
        dst = self.ring[s][:, 0:n * cols]
        if n > 1:
            dst = dst.rearrange("p (n c) -> p n c", n=n)
            P.dma("sp", dst, ap, reads=srcb, writes=[self.rbufs[s]])
        else:
            P.dma("sp", dst, ap, reads=srcb, writes=[self.rbufs[s]])

    def get(self, i, depth=None):
        hi = min(i + (self.depth if depth is None else depth), len(self.units) - 1)
        while self.issued <= hi:
            self._issue(self.issued)
            self.issued += 1
        s = i % len(self.ring)
        return self.ring[s], self.rbufs[s]


def build_program(NT=NT_FULL, stop_after="C", debug=()):
    assert NT % 1024 == 0
    NTILE = NT // TT
    R = NT // GW
    RSEG = R // 2
    NCH = NT // 128
    SEG = NT // 2
    nc = bass.Bass("TRN2", target_bir_lowering=False)

    def dram(name, shape, dt, kind="Internal"):
        if name in debug:
            kind = "ExternalOutput"
        return nc.dram_tensor(name, list(shape), dt, kind=kind).ap()

    x = dram("x", [NT, D], F32, "ExternalInput")
    y = dram("y", [NT, D], F32, "ExternalOutput")
    cont_d = dram("cont", [128, 1], F32, "ExternalInput")
    wshapes = {
        "wgu_a": [JC * 128, 4096], "wd_a": [KC * 128, DFF], "wgu_b": [JC * 128, 4096], "wd_b": [KC * 128, DFF],
        "win_fm": [64 * 128, 2048], "win_tm": [6 * 128, 8192], "win_gt": [128, 256],
        "wpa": [KC * 128, 1024], "wpm": [KC * 128, 1024], "wout": [KC * 128, 2048],
    }
    wf = {k: dram(k, s, F32, "ExternalInput") for k, s in wshapes.items()}
    wb = {k: dram(k + "_bf", s, BF16) for k, s in wshapes.items()}
    nrm_d = dram("nrm", [128, 96], F32, "ExternalInput")
    convw_d = dram("convw", [128, 80], F32, "ExternalInput")
    convb_d = dram("convb", [128, 16], F32, "ExternalInput")
    gateb_d = dram("gateb", [128, 16], F32, "ExternalInput")
    mlg_d = dram("mlg", [128, 1024], F32, "ExternalInput")
    rpbg_d = dram("rpbg", [512, 960], F32, "ExternalInput")
    cmask_d = dram("cmask", [64, 960], F32, "ExternalInput")
    convbr_d = dram("convbr", [1, 2048], F32, "ExternalInput")

    XA1 = dram("XA1", [D, NT], F32)
    QT = dram("QT", [1024, NT], BF16)
    KT = dram("KT", [1024, NT], BF16)
    QKR = dram("QKR", [2048, NT], BF16)
    GA = dram("GA", [2048, NT], BF16)
    GM = dram("GM", [2048, NT], BF16)
    VNA = dram("VNA", [NT, 1024], BF16)
    VML = dram("VML", [NT, 1024], BF16)
    OML = dram("OML", [NT, 1024], BF16)
    GATES = dram("GATES", [NT, 16], F32)
    YAT = dram("YAT", [1024, NT], BF16)
    YMT = dram("YMT", [1024, NT], BF16)
    HB = dram("HB", [NT, 1024], F32)

    with contextlib.ExitStack() as es:
        P = Prog(nc, es)
        outbufs = []

        ident_bf = P.sb("ident_bf", [128, 128], BF16)
        ident32 = P.sb("ident32", [128, 128], F32)
        onesD = P.sb("onesD", [128, 128], BF16)
        nrm = P.sb("nrm", [128, 96], F32)
        nrm_ab = P.sb("nrm_ab", [128, 96], F32)
        cont = P.sb("cont", [128, 1], F32)
        wgt = P.sb("wgt", [128, 256], BF16)
        bc = P.buf()
        P.op("pool", "memset", ident32[:], 0.0, writes=[bc])
        P.op("pool", "affine_select", ident32[:], ident32[:], [[-1, 128]], ALU.not_equal, 1.0, base=0,
             channel_multiplier=1, reads=[bc], writes=[bc])
        P.op("pool", "tensor_copy", ident_bf[:], ident32[:], reads=[bc], writes=[bc])
        P.op("pool", "memset", onesD[:], 1.0 / D, writes=[bc])
        P.dma("sp", nrm[:], nrm_d[:, :], writes=[bc])
        P.dma("sp", cont[:], cont_d[:, :], writes=[bc])
        P.op("dve", "tensor_scalar", nrm_ab[:], nrm[:], ALPHA, None, op0=ALU.mult, reads=[bc], writes=[bc])

        wbuf = {}
        def cast(name, rows_per):
            nrows = wshapes[name][0]
            lst = []
            for r0 in range(0, nrows, rows_per):
                b = P.buf()
                P.dma("pool", wb[name][r0:r0 + rows_per, :], wf[name][r0:r0 + rows_per, :], writes=[b])
                lst.append(b)
            wbuf[name] = lst
        cast("wgu_a", 128)
        cast("wd_a", 128)
        cast("win_fm", 512)
        cast("win_tm", 128)
        cast("win_gt", 128)
        P.dma("sp", wgt[:], wb["win_gt"][:, :], reads=wbuf["win_gt"], writes=[bc])

        def fm_view(ap2d):
            return ap2d.rearrange("(k p) t -> p k t", p=128)

        sb_xa1 = P.bufs(NTILE); sb_fm = P.bufs(NTILE); sb_tm = P.bufs(NTILE)
        sb_yat = P.buf(); sb_ymt = P.buf(); sb_hb = P.bufs(NCH)

        def make_tile_ctx(pes):
            c = {}
            c["R32"] = P.sb("R32", [128, KC, TT], F32, pes)
            c["Rb"] = P.sb("Rb", [128, KC, TT], BF16, pes)
            c["U"] = P.sb("U", [128, 64, TT], BF16, pes)
            c["ring"] = [P.sb(f"ring{i}", [128, 8192], BF16, pes) for i in range(3)]
            c["ringb"] = P.bufs(3)
            c["sg"] = [P.sb(f"sg{i}", [128, TT], BF16, pes) for i in range(2)]
            c["sgb"] = P.bufs(2)
            c["tb"] = [P.sb(f"tb{i}", [128, TT], BF16, pes) for i in range(2)]
            c["tbb"] = P.bufs(2)
            c["sq"] = [P.sb(f"sq{i}", [128, TT], BF16, pes) for i in range(2)]
            c["sqb"] = P.bufs(2)
            c["mean"] = P.sb("mean_sb", [128, TT], F32, pes)
            c["rstd"] = P.sb("rstd_sb", [128, TT], F32, pes)
            c["meanb"] = P.buf(); c["rstdb"] = P.buf()
            c["B"] = [P.ps(f"bank{i}", [128, TT], F32, pes) for i in range(8)]
            c["Bb"] = P.pbufs(8)
            c["R32b"] = [P.bufs(4) for _ in range(KC)]
            c["Rbb"] = [P.bufs(4) for _ in range(KC)]
            c["Ub"] = P.bufs(64)
            return c

        def ln_stats_chunk(c, f):
            R32, k = c["R32"], f % 2
            P.op("act", "activation", c["sq"][k][:], R32[:, f, :], AF.Square, reads=c["R32b"][f], writes=[c["sqb"][k]])
            P.op("dve", "tensor_copy", c["tb"][k][:], R32[:, f, :], reads=c["R32b"][f], writes=[c["tbb"][k]])

        def ln_stats_mm(c, f):
            k = f % 2
            P.op("pe", "matmul", c["B"][6][:], onesD[:], c["tb"][k][:], start=(f == 0), stop=(f == KC - 1),
                 reads=[c["tbb"][k], bc], writes=[c["Bb"][6]])
            P.op("pe", "matmul", c["B"][7][:], onesD[:], c["sq"][k][:], start=(f == 0), stop=(f == KC - 1),
                 reads=[c["sqb"][k], bc], writes=[c["Bb"][7]])

        def ln_finalize(c, gi, out_scale):
            R32, Rb, mean, rstd = c["R32"], c["Rb"], c["mean"], c["rstd"]
            P.op("act", "activation", mean[:], c["B"][6][:], AF.Copy, reads=[c["Bb"][6]], writes=[c["meanb"]])
            P.op("dve", "tensor_tensor", rstd[:], mean[:], mean[:], op=ALU.mult, reads=[c["meanb"]], writes=[c["rstdb"]])
            P.op("dve", "tensor_tensor", rstd[:], c["B"][7][:], rstd[:], op=ALU.subtract, reads=[c["Bb"][7], c["rstdb"]], writes=[c["rstdb"]])
            P.op("act", "activation", rstd[:], rstd[:], AF.Sqrt, bias=LN_EPS, scale=1.0, reads=[c["rstdb"]], writes=[c["rstdb"]])
            P.op("dve", "reciprocal", rstd[:], rstd[:], reads=[c["rstdb"]], writes=[c["rstdb"]])
            btile = nrm_ab if out_scale != 1.0 else nrm
            for f in range(KC):
                gcol = nrm[:, gi * 16 + f: gi * 16 + f + 1]
                bcol = nrm[:, (gi + 1) * 16 + f: (gi + 1) * 16 + f + 1]
                sbcol = btile[:, (gi + 1) * 16 + f: (gi + 1) * 16 + f + 1]
                P.op("dve", "tensor_tensor", R32[:, f, :], R32[:, f, :], mean[:], op=ALU.subtract,
                     reads=c["R32b"][f] + [c["meanb"]], writes=c["R32b"][f])
                P.op("dve", "scalar_tensor_tensor", R32[:, f, :], R32[:, f, :], gcol, rstd[:], op0=ALU.mult, op1=ALU.mult,
                     reads=c["R32b"][f] + [c["rstdb"], bc], writes=c["R32b"][f])
                P.op("act", "activation", Rb[:, f, :], R32[:, f, :], AF.Identity, bias=bcol, scale=1.0,
                     reads=c["R32b"][f] + [bc], writes=c["Rbb"][f])
                P.op("act", "activation", R32[:, f, :], R32[:, f, :], AF.Identity, bias=sbcol, scale=float(out_scale),
                     reads=c["R32b"][f] + [bc], writes=c["R32b"][f])

        def ffn_core(c, ws, u0):
            B, Bb, U, Ub, Rb = c["B"], c["Bb"], c["U"], c["Ub"], c["Rb"]
            for j in range(JC):
                W, Wb = ws.get(u0 + j // 2)
                jj = j % 2
                bg, bu = j % 2, 2 + j % 2
                for g, bk in ((0, bg), (1, bu)):
                    for kc in range(KC):
                        col = ((jj * KC + kc) * 2 + g) * 128
                        P.op("pe", "matmul", B[bk][:], W[:, col:col + 128], Rb[:, kc, :], start=(kc == 0), stop=(kc == KC - 1),
                             reads=[Wb] + c["Rbb"][kc], writes=[Bb[bk]])
                k = j % 2
                P.op("act", "activation", c["sg"][k][:], B[bg][:], AF.Silu, reads=[Bb[bg]], writes=[c["sgb"][k]])
                P.op("dve", "tensor_tensor", U[:, j, :], c["sg"][k][:], B[bu][:], op=ALU.mult,
                     reads=[c["sgb"][k], Bb[bu]], writes=[Ub[j]])
            for f in range(KC):
                W, Wb = ws.get(u0 + 22 + f)
                bk = 4 + f % 2
                for kc in range(JC):
                    P.op("pe", "matmul", B[bk][:], W[:, kc * 128:(kc + 1) * 128], U[:, kc, :], start=(kc == 0), stop=(kc == JC - 1),
                         reads=[Wb, Ub[kc]], writes=[Bb[bk]])
                if f > 0:
                    ln_stats_mm(c, f - 1)
                P.op("dve", "scalar_tensor_tensor", c["R32"][:, f, :], B[bk][:], 0.5, c["R32"][:, f, :], op0=ALU.mult, op1=ALU.add,
                     reads=[Bb[bk]] + c["R32b"][f], writes=c["R32b"][f])
                ln_stats_chunk(c, f)
            ln_stats_mm(c, KC - 1)

        def ffn_units(name_gu, name_d):
            us = []
            for u in range(22):
                us.append((wb[name_gu][u * 256:(u + 1) * 256, :].rearrange("(n p) c -> p n c", p=128), 2, 4096,
                           wbuf[name_gu][2 * u:2 * u + 2]))
            for f in range(KC):
                us.append((wb[name_d][f * 128:(f + 1) * 128, :], 1, DFF, [wbuf[name_d][f]]))
            return us

        if stop_after == "W":
            P.final_wait(sum(wbuf.values(), []))
            P.emit()
            return nc
        with contextlib.ExitStack() as pes:
          try:
            c = make_tile_ctx(pes)
            B, Bb, U, Ub, R32, Rb = c["B"], c["Bb"], c["U"], c["Ub"], c["R32"], c["Rb"]
            xs = [P.sb(f"xs{i}", [128, D], F32, pes) for i in range(2)]
            xsb = P.bufs(2)
            gst = P.sb("gst", [128, 4, 16], F32, pes)
            gstb = P.buf()
            gateb = P.sb("gateb", [128, 16], F32, pes)
            P.dma("sp", gateb[:], gateb_d[:, :], writes=[bc])
            units = []
            for t in range(NTILE):
                units += ffn_units("wgu_a", "wd_a")
                for u in range(16):
                    units.append((wb["win_fm"][u * 512:(u + 1) * 512, :].rearrange("(n p) c -> p n c", p=128), 4, 2048,
                                  [wbuf["win_fm"][u]]))
                for g in range(6):
                    units.append((wb["win_tm"][g * 128:(g + 1) * 128, :], 1, 8192, [wbuf["win_tm"][g]]))
            UPT = 38 + 16 + 6
            ws = WStream(P, c["ring"], c["ringb"], units)
            fm_dst = [(QT, 0), (QT, 4), (KT, 0), (KT, 4), (QKR, 0), (QKR, 4), (QKR, 8), (QKR, 12),
                      (GA, 0), (GA, 4), (GA, 8), (GA, 12), (GM, 0), (GM, 4), (GM, 8), (GM, 12)]
            tm_dst = [VNA, VNA, VML, VML, OML, OML]
            for t in range(NTILE):
                tok0 = t * TT
                _chk("A:load0")
                for tb in range(4):
                    k2 = tb % 2
                    P.dma("sp", xs[k2][:], x[tok0 + tb * 128: tok0 + (tb + 1) * 128, :], writes=[xsb[k2]])
                    _chk("A:load1")
                    for b in range(4):
                        bank3 = B[b][:, :].rearrange("p (k t) -> p k t", k=4)
                        for k in range(4):
                            kc = 4 * b + k
                            P.op("pe", "transpose", bank3[:, k, :], xs[k2][:, kc * 128:(kc + 1) * 128], ident32[:],
                                 reads=[xsb[k2], bc], writes=[Bb[b]])
                        P.op("act", "activation", R32[:, 4 * b:4 * b + 4, tb * 128:(tb + 1) * 128], bank3, AF.Copy, scale=ALPHA,
                             reads=[Bb[b]], writes=[c["R32b"][4 * b + k][tb] for k in range(4)])
                        P.op("dve", "tensor_copy", Rb[:, 4 * b:4 * b + 4, tb * 128:(tb + 1) * 128], bank3,
                             reads=[Bb[b]], writes=[c["Rbb"][4 * b + k][tb] for k in range(4)])
                _chk("A:load")
                ffn_core(c, ws, t * UPT)
                _chk("A:ffn")
                ln_finalize(c, 0, ALPHA)
                _chk("A:ln")
                P.dma("pool", fm_view(XA1)[:, :, tok0:tok0 + TT], R32[:], reads=sum(c["R32b"], []), writes=[sb_xa1[t]])
                for u in range(16):
                    W, Wb = ws.get(t * UPT + 38 + u)
                    sblk = 44 + 4 * (u % 5)
                    for cl in range(4):
                        bk = (4 * u + cl) % 4
                        for kc in range(KC):
                            col = (cl * KC + kc) * 128
                            P.op("pe", "matmul", B[bk][:], W[:, col:col + 128], Rb[:, kc, :], start=(kc == 0), stop=(kc == KC - 1),
                                 reads=[Wb] + c["Rbb"][kc], writes=[Bb[bk]])
                        dst = U[:, sblk + cl, :]
                        if u < 2:
                            P.op("dve", "tensor_scalar", dst, B[bk][:], 128.0 ** -0.5, None, op0=ALU.mult, reads=[Bb[bk]], writes=[Ub[sblk + cl]])
                        elif u < 8:
                            P.op("dve", "tensor_copy", dst, B[bk][:], reads=[Bb[bk]], writes=[Ub[sblk + cl]])
                        else:
                            P.op("act", "activation", dst, B[bk][:], AF.Sigmoid, reads=[Bb[bk]], writes=[Ub[sblk + cl]])
                    dten, dc0 = fm_dst[u]
                    P.dma("pool", fm_view(dten)[:, dc0:dc0 + 4, tok0:tok0 + TT], U[:, sblk:sblk + 4, :],
                          reads=Ub[sblk:sblk + 4], writes=[sb_fm[t]])
                _chk("A:fm")
                for g in range(6):
                    W, Wb = ws.get(t * UPT + 54 + g)
                    for tb in range(4):
                        bk = (4 * g + tb) % 4
                        for kc in range(KC):
                            P.op("pe", "matmul", B[bk][:], Rb[:, kc, tb * 128:(tb + 1) * 128], W[:, kc * 512:(kc + 1) * 512],
                                 start=(kc == 0), stop=(kc == KC - 1), reads=[Wb] + c["Rbb"][kc], writes=[Bb[bk]])
                        sblk = 44 + (4 * g + tb) % 20
                        if g < 4:
                            P.op("dve", "tensor_copy", U[:, sblk, :], B[bk][:], reads=[Bb[bk]], writes=[Ub[sblk]])
                        else:
                            P.op("act", "activation", U[:, sblk, :], B[bk][:], AF.Sigmoid, reads=[Bb[bk]], writes=[Ub[sblk]])
                        P.dma("pool", tm_dst[g][tok0 + tb * 128: tok0 + (tb + 1) * 128, (g % 2) * 512:(g % 2) * 512 + 512],
                              U[:, sblk, :], reads=[Ub[sblk]], writes=[sb_tm[t]])
                _chk("A:tm")
                for tb in range(4):
                    bk = tb
                    for kc in range(KC):
                        P.op("pe", "matmul", B[bk][:, 0:16], Rb[:, kc, tb * 128:(tb + 1) * 128], wgt[:, kc * 16:(kc + 1) * 16],
                             start=(kc == 0), stop=(kc == KC - 1), reads=[bc] + c["Rbb"][kc], writes=[Bb[bk]])
                    P.op("dve", "tensor_tensor", gst[:, tb, :], B[bk][:, 0:16], gateb[:], op=ALU.add, reads=[Bb[bk], bc], writes=[gstb])
                P.dma("pool", GATES[tok0:tok0 + TT, :].rearrange("(b p) g -> p b g", p=128), gst[:], reads=[gstb], writes=[sb_tm[t]])
                _chk("A:gt")
            P.barrier()
          except _Stop:
            P.dma("pool", fm_view(XA1)[:, :, 0:TT], c["R32"][:], reads=sum(c["R32b"], []), writes=[sb_xa1[0]])
            P.final_wait([sb_xa1[0]])
            P.emit()
            return nc
        if stop_after == "A":
            outbufs += sb_xa1 + sb_fm + sb_tm
        cast("wpa", 1024)
        cast("wpm", 1024)
        cast("wout", 512)
        cast("wgu_b", 128)
        cast("wd_b", 128)

        if stop_after in ("NA", "ML", "C"):
            with contextlib.ExitStack() as pes:
                Kh = [P.sb(f"Kh{i}", [128, NT], BF16, pes) for i in range(2)]
                Qh = [P.sb("Qh0", [128, NT], BF16, pes)] * 2
                Vh = [P.sb("Vh0", [64, R, 128], BF16, pes)] * 2
                Yh = [P.sb("Yh0", [128, NT], BF16, pes)] * 2
                Khb = P.bufs(2)
                Qhb = [P.buf()] * 2
                Vhb = [P.buf()] * 2
                Yhb = [P.buf()] * 2
                bias_bf = P.sb("bias_bf", [64, 8, 960], BF16, pes)
                S4 = P.ps("S4", [128, 4, 512], F32, pes)
                PT2 = P.ps("PT2", [64, 16, 128], BF16, pes)
                PTp = [PT2, PT2]
                Ops = [P.ps(f"Ops{i}", [128, 512], F32, pes) for i in range(2)]
                Spb, Opb = P.pbufs(4), P.pbufs(2)
                PTpb = [P.pbuf()] * 2
                ND = 4
                Pe = [P.sb(f"Pe{i}", [128, 768], F32, pes) for i in range(ND)]
                Pn = [P.sb(f"Pn{i}", [128, 768], BF16, pes) for i in range(ND)]
                PTs = [P.sb(f"PTs{i}", [64, 12, 128], BF16, pes) for i in range(ND)]
                st4 = [P.sb(f"st4{i}", [128, 4], F32, pes) for i in range(ND)]
                bias2 = P.sb("bias2", [128, 8, 832], BF16, pes)
                Peb, Pnb, PTsb, st4b = P.bufs(ND), P.bufs(ND), P.bufs(ND), P.bufs(ND)
                rows = []
                for r in range(R):
                    seg, rl = r // RSEG, r % RSEG
                    rs_s = seg * RSEG + min(max(rl - 4, 0), RSEG - 8)
                    rs_p = min(max(r - 4, 0), R - 8)
                    rows.append((rs_s, rs_p))
                seam_rows = [r for r in range(R) if rows[r][0] != rows[r][1]]
                SM = {}
                for r in seam_rows:
                    rs_s, rs_p = rows[r]
                    n_ = max(rs_s, rs_p) + 8 - min(rs_s, rs_p)
                    SM[r] = P.sb(f"SM{r}", [64, n_ * 64], BF16, pes)
                with contextlib.ExitStack() as tes:
                    rp = [P.sb(f"rpbg_s{i}", [128, 960], F32, tes) for i in range(2)]
                    rpb_ = P.bufs(2)
                    cm = P.sb("cmask_s", [128, 960], F32, tes)
                    rmI = P.sb("rmI", [128, 960], F32, tes)
                    tI = P.sb("tI", [128, 960], F32, tes)
                    tIb = P.buf()
                    ms = P.sb("ms_t", [64, 768], F32, tes)
                    md = P.sb("md_t", [64, 768], F32, tes)
                    tmpb = P.buf()
                    mb = P.buf()
                    P.dma("sp", cm[0:64, :], cmask_d[:, :], writes=[tmpb])
                    P.dma("sp", cm[64:128, :], cmask_d[:, :], writes=[tmpb])
                    P.op("pool", "memset", rmI[:, :], NEG, writes=[tmpb])
                    P.op("pool", "memset", rmI[:, 3 * 64:11 * 64], 0.0, writes=[tmpb])
                    P.op("dve", "tensor_tensor", rmI[:], rmI[:], cm[:], op=ALU.add, reads=[tmpb], writes=[tmpb])
                    for h in range(8):
                        P.dma("sp", rp[h % 2][0:64, :], rpbg_d[h * 64:(h + 1) * 64, :], writes=[rpb_[h % 2]])
                        P.dma("sp", rp[h % 2][64:128, :], rpbg_d[h * 64:(h + 1) * 64, :], writes=[rpb_[h % 2]])
                        P.op("dve", "tensor_tensor", bias_bf[:, h, :], rp[h % 2][0:64, :], cm[0:64, :], op=ALU.add, reads=[tmpb, rpb_[h % 2]], writes=[bc])
                        P.op("dve", "tensor_tensor", tI[:], rp[h % 2][:], rmI[:], op=ALU.add, reads=[tmpb, rpb_[h % 2]], writes=[tIb])
                        P.op("dve", "tensor_copy", bias2[0:64, h, 0:832], tI[0:64, 0:832], reads=[tIb], writes=[bc])
                        P.op("dve", "tensor_copy", bias2[64:128, h, 64:832], tI[64:128, 0:768], reads=[tIb], writes=[bc])
                    for r in seam_rows:
                        rs_s, rs_p = rows[r]
                        lo, hi = min(rs_s, rs_p), max(rs_s, rs_p) + 8
                        n = hi - lo
                        smt = SM[r]
                        for a in range(n):
                            vs = 0.0 if rs_s <= lo + a < rs_s + 8 else NEG
                            vp = 0.0 if rs_p <= lo + a < rs_p + 8 else NEG
                            P.op("pool", "memset", ms[:, a * 64:(a + 1) * 64], vs, writes=[mb])
                            P.op("pool", "memset", md[:, a * 64:(a + 1) * 64], vp - vs, writes=[mb])
                        P.op("dve", "scalar_tensor_tensor", smt[:], md[:, 0:n * 64], cont[0:64, 0:1], ms[:, 0:n * 64], op0=ALU.mult, op1=ALU.add,
                             reads=[mb, bc], writes=[bc])
                    P.barrier()
                bc = P.buf()
                items = []
                scnt = 0
                def interior(r):
                    return 0 <= r < R and rows[r][0] == rows[r][1] and rows[r][0] == r - 4

                for h in range(8):
                    skip = False
                    for r in range(R):
                        if skip:
                            skip = False
                            continue
                        rs_s, rs_p = rows[r]
                        lo, hi = min(rs_s, rs_p), max(rs_s, rs_p) + 8
                        n = hi - lo
                        pair = interior(r) and interior(r + 1)
                        if pair:
                            n = 9
                            skip = True
                        it = {"h": h, "r": r, "lo": lo, "n": n, "ncol": n * 64, "hb": h % 2, "i": len(items), "pair": pair,
                              "np": 128 if pair else 64, "last": (r == R - 1) or (pair and r + 1 == R - 1)}
                        if n * 64 > 512:
                            sk = scnt + (scnt % 2)
                            scnt = sk + 2
                            it["sk"] = (sk % 4, 2)
                        else:
                            it["sk"] = (scnt % 4, 1)
                            scnt += 1
                        items.append(it)

                def S_of(it):
                    sk, w = it["sk"]
                    npp = it["np"]
                    if w == 2:
                        return S4[0:npp, sk:sk + 2, :].rearrange("p a c -> p (a c)"), [Spb[sk], Spb[sk + 1]]
                    return S4[0:npp, sk, :], [Spb[sk]]

                def stA(it):
                    h, r, lo, ncol, hb_ = it["h"], it["r"], it["lo"], it["ncol"], it["hb"]
                    if r == 0:
                        P.dma("sp", Kh[hb_][:], KT[h * 128:(h + 1) * 128, :], reads=sb_fm, writes=[Khb[hb_]])
                        P.dma("sp", Qh[hb_][:], QT[h * 128:(h + 1) * 128, :], reads=sb_fm, writes=[Qhb[hb_]])
                    S, Sb_ = S_of(it)
                    segs = [(0, min(512, ncol))] + ([(512, ncol)] if ncol > 512 else [])
                    boff = (lo - r + 7) * 64
                    npp = it["np"]
                    for (c0, c1) in segs:
                        P.op("pe", "matmul", S[:, c0:c1], Qh[hb_][:, r * 64:r * 64 + npp], Kh[hb_][:, lo * 64 + c0: lo * 64 + c1],
                             start=True, stop=False, reads=[Qhb[hb_], Khb[hb_]], writes=Sb_)
                        if it["pair"]:
                            P.op("pe", "matmul", S[:, c0:c1], ident_bf[:, :], bias2[:, h, 192 + c0: 192 + c1],
                                 start=False, stop=True, reads=[bc], writes=Sb_)
                            continue
                        P.op("pe", "matmul", S[:, c0:c1], ident_bf[0:64, 0:64], bias_bf[:, h, boff + c0: boff + c1],
                             start=False, stop=(r not in SM), reads=[bc], writes=Sb_)
                        if r in SM:
                            P.op("pe", "matmul", S[:, c0:c1], ident_bf[0:64, 0:64], SM[r][:, c0:c1],
                                 start=False, stop=True, reads=[bc], writes=Sb_)

                def stB1(it):
                    k, ncol, q = it["i"] % ND, it["ncol"], it["np"]
                    S, Sb_ = S_of(it)
                    P.op("dve", "tensor_reduce", st4[k][0:q, 0:1], S[:, 0:ncol], axis=AX.X, op=ALU.max, reads=Sb_, writes=[st4b[k]])
                    P.op("dve", "tensor_scalar", st4[k][0:q, 1:2], st4[k][0:q, 0:1], -1.0, None, op0=ALU.mult, reads=[st4b[k]], writes=[st4b[k]])
                    P.op("act", "activation", Pe[k][0:q, 0:ncol], S[:, 0:ncol], AF.Exp, bias=st4[k][0:q, 1:2], scale=1.0,
                         accum_out=st4[k][0:q, 2:3], reads=Sb_ + [st4b[k]], writes=[Peb[k], st4b[k]])

                def stB2(it):
                    k, ncol, q = it["i"] % ND, it["ncol"], it["np"]
                    P.op("dve", "reciprocal", st4[k][0:q, 3:4], st4[k][0:q, 2:3], reads=[st4b[k]], writes=[st4b[k]])
                    P.op("act", "activation", Pn[k][0:q, 0:ncol], Pe[k][0:q, 0:ncol], AF.Copy, scale=st4[k][0:q, 3:4],
                         reads=[Peb[k], st4b[k]], writes=[Pnb[k]])

                def stC(it):
                    k, k2, n = it["i"] % ND, it["i"] % 2, it["n"]
                    q = it["np"]
                    for a in range(n):
                        P.op("pe", "transpose", PTp[k2][:, a, 0:q], Pn[k][0:q, a * 64:(a + 1) * 64], ident_bf[0:q, 0:q],
                             reads=[Pnb[k], bc], writes=[PTpb[k2]])
                    P.op("dve", "tensor_copy", PTs[k][:, 0:n, 0:q], PTp[k2][:, 0:n, 0:q], reads=[PTpb[k2]], writes=[PTsb[k]])

                def stD(it):
                    k, k2, n, lo, r, h, hb_ = it["i"] % ND, it["i"] % 2, it["n"], it["lo"], it["r"], it["h"], it["hb"]
                    if r == 0:
                        P.dma("sp", Vh[hb_][:], VNA.rearrange("(r p) c -> p r c", p=64)[:, :, h * 128:(h + 1) * 128],
                              reads=sb_tm, writes=[Vhb[hb_]])
                    q = it["np"]
                    for a in range(n):
                        P.op("pe", "matmul", Ops[k2][:, 0:q], Vh[hb_][:, lo + a, :], PTs[k][:, a, 0:q], start=(a == 0), stop=(a == n - 1),
                             reads=[Vhb[hb_], PTsb[k]], writes=[Opb[k2]])
                    if it["i"] % 2 == 0:
                        P.op("dve", "tensor_copy", Yh[hb_][:, r * 64:r * 64 + q], Ops[k2][:, 0:q], reads=[Opb[k2]], writes=[Yhb[hb_]])
                    else:
                        P.op("act", "activation", Yh[hb_][:, r * 64:r * 64 + q], Ops[k2][:, 0:q], AF.Copy, reads=[Opb[k2]], writes=[Yhb[hb_]])
                    if it["last"]:
                        P.dma("pool", YAT[h * 128:(h + 1) * 128, :], Yh[hb_][:], reads=[Yhb[hb_]], writes=[sb_yat])

                NI = len(items)
                for i in range(NI + 3):
                    if i < NI:
                        stA(items[i])
                        stB1(items[i])
                    if 0 <= i - 1 < NI:
                        stB2(items[i - 1])
                    if 0 <= i - 2 < NI:
                        stC(items[i - 2])
                    if 0 <= i - 3 < NI:
                        stD(items[i - 3])
                P.barrier()
            bc = P.buf()
        if stop_after == "NA":
            outbufs += [sb_yat]

        if stop_after in ("ML", "C"):
            with contextlib.ExitStack() as pes:
              try:
                  SC = 512
                  NSC = NT // SC
                  qkr = [P.sb(f"qkr{i}", [128, 16, SC + 4], BF16, pes) for i in range(2)]
                  qkrb = P.bufs(2)
                  Va = [P.sb(f"Va{i}", [128, 4, 257], BF16, pes) for i in range(6)]
                  Vab = P.bufs(6)
                  Om = [P.sb(f"Om{i}", [128, 1024], BF16, pes) for i in range(6)]
                  Omb = P.bufs(6)
                  Gt = [P.sb(f"Gt{i}", [128, 16], F32, pes) for i in range(6)]
                  Gtb = P.bufs(6)
                  HBl = [P.sb(f"HBl{i}", [128, 1024], F32, pes) for i in range(2)]
                  HBlb = P.bufs(2)
                  HS = [P.sb(f"HS{i}", [128, 1024], F32, pes) for i in range(2)]
                  HSb = P.bufs(2)
                  qkT = [P.sb(f"qkT{i}", [128, 16, SC], BF16, pes) for i in range(2)]
                  qkTb = [P.bufs(16) for _ in range(2)]
                  DG = P.sb("DG", [128, 80, 128], BF16, pes)
                  convw = P.sb("convw", [128, 80], F32, pes)
                  convb = P.sb("convb", [128, 16], F32, pes)
                  cbr32 = P.sb("cbr32", [1, 2048], F32, pes)
                  cbr = P.sb("cbr", [1, 2048], BF16, pes)
                  onesr = P.sb("onesr", [1, SC], BF16, pes)
                  mlg = P.sb("mlg", [128, 1024], F32, pes)
                  tri_le = P.sb("tri_le", [128, 128], F32, pes)
                  tri_ge = P.sb("tri_ge", [128, 128], F32, pes)
                  ones32 = P.sb("ones32", [128, 128], F32, pes)
                  mask_f = P.sb("mask_f", [128, 128], BF16, pes)
                  mask_b = P.sb("mask_b", [128, 128], BF16, pes)
                  C32 = [P.sb(f"C32_{h}", [128, 2, 257], F32, pes) for h in range(4)]
                  Cbf = [P.sb(f"Cbf_{h}", [128, 2, 257], BF16, pes) for h in range(4)]
                  C32b, Cbfb = P.bufs(4), P.bufs(4)
                  lsp = [P.sb(f"lsp{i}", [128, 8], F32, pes) for i in range(6)]
                  EX = [P.sb(f"EX{i}", [128, 32], F32, pes) for i in range(6)]
                  EE = [P.sb(f"EE{i}", [128, 40], F32, pes) for i in range(6)]
                  lspb, EXb, EEb = P.bufs(6), P.bufs(6), P.bufs(6)
                  scT = [[P.sb(f"scT{i}_{h}", [128, 128], BF16, pes) for h in range(4)] for i in range(2)]
                  kw = [[P.sb(f"kw{i}_{h}", [128, 256], BF16, pes) for h in range(4)] for i in range(2)]
                  scTb, kwb = [P.bufs(4) for _ in range(2)], [P.bufs(4) for _ in range(2)]
                  sm = [P.sb(f"sm{i}", [128, 12], F32, pes) for i in range(2)]
                  smb = P.bufs(2)
                  HR = [P.sb(f"HR{i}", [128, 4, 257], F32, pes) for i in range(2)]
                  HRb = P.bufs(2)
                  bst = P.sb("bst", [128, 4, 6], F32, pes)
                  bmv = P.sb("bmv", [128, 4, 2], F32, pes)
                  brs = P.sb("brs", [128, 4], F32, pes)
                  bstb = P.buf()
                  YN = P.sb("YN", [128, 1024], F32, pes)
                  GO = P.sb("GO", [128, 1024], F32, pes)
                  YM = P.sb("YM", [128, 1024], BF16, pes)
                  YNb, GOb, YMb = P.buf(), P.buf(), P.buf()
                  YMTs = [P.sb(f"YMTs{i}", [128, 8, SC], BF16, pes) for i in range(2)]
                  YMTsb = P.bufs(2)
                  pcv = [P.ps(f"pcv{i}", [128, SC], F32, pes) for i in range(2)]
                  pcvb = P.pbufs(2)
                  AXp = P.ps("AXp", [128, 512], F32, pes)
                  Ab = P.pbuf()
                  csb = Ab
                  kTT = P.ps("kTT", [128, 8, 128], BF16, pes)
                  kTTb = P.pbuf()
                  NUMs = [P.ps(f"NUM{i}", [128, 512], F32, pes) for i in range(2)]
                  NUMbs = P.pbufs(2)
                  NUM, NUMb = NUMs[0], NUMbs[0]
                  Ups = P.ps("Ups", [128, 2, 512], F32, pes)
                  Upsb = P.pbuf()
                  ymp, ympb = kTT, kTTb
                  P.dma("sp", convw[:], convw_d[:, :], writes=[bc])
                  P.dma("sp", convb[:], convb_d[:, :], writes=[bc])
                  P.dma("sp", cbr32[:], convbr_d[:, :], writes=[bc])
                  P.op("dve", "tensor_copy", cbr[:], cbr32[:], reads=[bc], writes=[bc])
                  P.op("pool", "memset", onesr[:], 1.0, writes=[bc])
                  P.dma("sp", mlg[:], mlg_d[:, :], writes=[bc])
                  P.op("pool", "memset", ones32[:], 1.0, writes=[bc])
                  P.op("pool", "affine_select", tri_le[:], ones32[:], [[1, 128]], ALU.is_ge, 0.0, base=0, channel_multiplier=-1,
                       reads=[bc], writes=[bc])
                  P.op("pool", "affine_select", tri_ge[:], ones32[:], [[-1, 128]], ALU.is_ge, 0.0, base=0, channel_multiplier=1,
                       reads=[bc], writes=[bc])
                  P.op("pool", "tensor_copy", mask_f[:], tri_le[:], reads=[bc], writes=[bc])
                  P.op("pool", "tensor_copy", mask_b[:], tri_ge[:], reads=[bc], writes=[bc])
                  for i in range(80):
                      P.op("dve", "tensor_scalar", DG[:, i, :], ident_bf[:], convw[:, i:i + 1], None, op0=ALU.mult, reads=[bc], writes=[bc])
                  for i in range(6):
                      P.op("pool", "memset", Va[i][:], 1.0, writes=[Vab[i]])
                  LN16 = math.log(16.0)
                  _chk('ML:const')

                  def load_sc(sc, k):
                      s0 = sc * SC
                      t_ = qkr[k]
                      src = fm_view(QKR)
                      P.dma("sp", t_[:, :, 2:SC + 2], src[:, :, s0:s0 + SC], reads=sb_fm, writes=[qkrb[k]])
                      for side in (0, 1):
                          cols = slice(0, 2) if side == 0 else slice(SC + 2, SC + 4)
                          edge = s0 if side == 0 else s0 + SC
                          if edge == 0 or edge == NT:
                              P.op("pool", "memset", t_[:, :, cols], 0.0, writes=[qkrb[k]])
                          else:
                              d0 = edge - 2 if side == 0 else edge
                              P.dma("sp", t_[:, :, cols], src[:, :, d0:d0 + 2], reads=sb_fm, writes=[qkrb[k]])
                              if edge == SEG:
                                  P.op("pool", "tensor_scalar", t_[:, :, cols], t_[:, :, cols], cont[:, 0:1], None, op0=ALU.mult,
                                       reads=[qkrb[k], bc], writes=[qkrb[k]])

                  def seam_scale():
                      for h in range(4):
                          P.op("dve", "tensor_scalar", C32[h][:], C32[h][:], cont[:, 0:1], None, op0=ALU.mult,
                               reads=[C32b[h], bc], writes=[C32b[h]])
                          P.op("pool", "tensor_copy", Cbf[h][:], C32[h][:], reads=[C32b[h]], writes=[Cbfb[h]])

                  items = []
                  for d in (1, 0):
                      sc_order = list(range(NSC)) if d == 0 else list(range(NSC - 1, -1, -1))
                      for sci, sc in enumerate(sc_order):
                          ch_order = list(range(4)) if d == 0 else [3, 2, 1, 0]
                          for ci, cl in enumerate(ch_order):
                              items.append({"d": d, "sc": sc, "k2": (len(items) // 4) % 2, "cl": cl, "ch": sc * 4 + cl, "first_sc": ci == 0,
                                            "last_sc": ci == 3, "first_sweep": sci == 0 and ci == 0, "g2": len(items) % 2, "g6": len(items) % 6, "j": len(items)})

                  def G1(it):
                      d, sc, k2, cl, ch, g6 = it["d"], it["sc"], it["k2"], it["cl"], it["ch"], it["g6"]
                      t0 = ch * 128
                      if it["first_sc"]:
                          load_sc(sc, k2)
                      P.dma("sp", Va[g6][:, :, 0:256], VML[t0:t0 + 128, :].rearrange("p (h e) -> p h e", h=4),
                            reads=sb_tm, writes=[Vab[g6]])
                      P.dma("sp", Gt[g6][:], GATES[t0:t0 + 128, :], reads=sb_tm, writes=[Gtb[g6]])
                      if d == 0:
                          P.dma("sp", Om[g6][:], OML[t0:t0 + 128, :], reads=sb_tm, writes=[Omb[g6]])
                      if it["first_sc"]:
                          for fc in range(16):
                              pb = pcv[fc % 2]
                              for j in range(5):
                                  P.op("pe", "matmul", pb[:, :], DG[:, fc * 5 + j, :], qkr[k2][:, fc, j: j + SC],
                                       start=(j == 0), stop=False, reads=[bc, qkrb[k2]], writes=[pcvb[fc % 2]])
                              P.op("pe", "matmul", pb[:, :], cbr[0:1, fc * 128:(fc + 1) * 128], onesr[0:1, :],
                                   start=False, stop=True, reads=[bc], writes=[pcvb[fc % 2]])
                              P.op("act", "activation", qkT[k2][:, fc, :], pb[:, :], AF.Silu,
                                   reads=[pcvb[fc % 2]], writes=[qkTb[k2][fc]])
                      P.op("act", "activation", lsp[g6][:], Gt[g6][:, 8:16], AF.Exp, scale=-1.0, reads=[Gtb[g6]], writes=[lspb[g6]])
                      P.op("act", "activation", lsp[g6][:], lsp[g6][:], AF.Ln, bias=1.0, scale=1.0, reads=[lspb[g6]], writes=[lspb[g6]])

                  def G2(it):
                      g6 = it["g6"]
                      P.op("pe", "matmul", NUM[:, 320:324], tri_le[:], lsp[g6][:, 0:4], start=True, stop=True, reads=[bc, lspb[g6]], writes=[NUMb])
                      P.op("pe", "matmul", NUM[:, 324:328], tri_ge[:], lsp[g6][:, 4:8], start=True, stop=True, reads=[bc, lspb[g6]], writes=[NUMb])
                      P.op("pe", "matmul", NUM[:, 328:336], ones32[:], lsp[g6][:, 0:8], start=True, stop=True, reads=[bc, lspb[g6]], writes=[NUMb])
                      P.op("dve", "scalar_tensor_tensor", EX[g6][:, 0:8], NUM[:, 320:328], -LN16, Gt[g6][:, 0:8], op0=ALU.add, op1=ALU.add,
                           reads=[NUMb, Gtb[g6]], writes=[EXb[g6]])
                      P.op("dve", "tensor_scalar", EX[g6][:, 8:24], NUM[:, 320:336], -1.0, None, op0=ALU.mult, reads=[NUMb], writes=[EXb[g6]])
                      P.op("dve", "tensor_copy", EX[g6][:, 24:32], NUM[:, 320:328], reads=[NUMb], writes=[EXb[g6]])

                  def G3(it):
                      g6 = it["g6"]
                      P.op("act", "activation", EE[g6][:, 0:32], EX[g6][:], AF.Exp, reads=[EXb[g6]], writes=[EEb[g6]])
                      P.op("dve", "tensor_tensor", EE[g6][:, 32:40], EE[g6][:, 0:8], EE[g6][:, 16:24], op=ALU.mult, reads=[EEb[g6]], writes=[EEb[g6]])

                  def G4(it):
                      d, k2, cl, g2, g6 = it["d"], it["k2"], it["cl"], it["g2"], it["g6"]
                      for h in range(4):
                          for dc in range(2):
                              P.op("pe", "matmul", AXp[:, h * 128:(h + 1) * 128], qkT[k2][:, 8 + 2 * h + dc, cl * 128:(cl + 1) * 128], qkT[k2][:, 2 * h + dc, cl * 128:(cl + 1) * 128],
                                   start=(dc == 0), stop=(dc == 1), reads=[qkTb[k2][8 + 2 * h + dc], qkTb[k2][2 * h + dc]], writes=[Ab])
                      for h in range(4):
                          for dc in range(2):
                              P.op("pe", "transpose", kTT[:, 2 * h + dc, :], qkT[k2][:, 8 + 2 * h + dc, cl * 128:(cl + 1) * 128], ident_bf[:],
                                   reads=[qkTb[k2][8 + 2 * h + dc], bc], writes=[kTTb])
                      for h in range(4):
                          col = d * 4 + h
                          P.op("dve", "scalar_tensor_tensor", scT[g2][h][:], AXp[:, h * 128:(h + 1) * 128], EE[g6][:, col:col + 1],
                               (mask_f if d == 0 else mask_b)[:], op0=ALU.mult, op1=ALU.mult,
                               reads=[Ab, EEb[g6], bc], writes=[scTb[g2][h]])
                          P.op("act", "activation", kw[g2][h][:], kTT[:, 2 * h:2 * h + 2, :], AF.Identity, bias=0.0, scale=EE[g6][:, 32 + col:33 + col],
                               reads=[kTTb, EEb[g6]], writes=[kwb[g2][h]])

                  def S2(it, nxt):
                      d, sc, k2, cl, ch, g2, g6 = it["d"], it["sc"], it["k2"], it["cl"], it["ch"], it["g2"], it["g6"]
                      t0 = ch * 128
                      if it["first_sweep"]:
                          for h in range(4):
                              P.op("pool", "memset", C32[h][:], 0.0, writes=[C32b[h]])
                              P.op("pool", "memset", Cbf[h][:], 0.0, writes=[Cbfb[h]])
                      if (d == 0 and ch == NCH // 2) or (d == 1 and ch == NCH // 2 - 1):
                          seam_scale()
                      for h in range(4):
                          col = d * 4 + h
                          hk = h % 2
                          NUMh, NUMhb = NUMs[h % 2], NUMbs[h % 2]
                          P.op("pe", "matmul", NUMh[:, 0:257], scT[g2][h][:], Va[g6][:, h, :], start=True, stop=False,
                               reads=[scTb[g2][h], Vab[g6]], writes=[NUMhb])
                          for dc in range(2):
                              P.op("pe", "matmul", NUMh[:, 0:257], qkT[k2][:, 2 * h + dc, cl * 128:(cl + 1) * 128], Cbf[h][:, dc, :], start=False, stop=(dc == 1),
                                   reads=[qkTb[k2][2 * h + dc], Cbfb[h]], writes=[NUMhb])
                          for dc in range(2):
                              P.op("pe", "matmul", Ups[:, dc, 0:257], kw[g2][h][:, dc * 128:(dc + 1) * 128], Va[g6][:, h, :], start=True, stop=True,
                                   reads=[kwb[g2][h], Vab[g6]], writes=[Upsb])
                          P.op("dve", "scalar_tensor_tensor", C32[h][:], C32[h][:], EE[g6][:, 16 + col:17 + col], Ups[:, :, 0:257],
                               op0=ALU.mult, op1=ALU.add, reads=[C32b[h], EEb[g6], Upsb], writes=[C32b[h]])
                          P.op("act", "activation", HR[g2][:, h, :], NUMh[:, 0:257], AF.Copy, reads=[NUMhb], writes=[HRb[g2]])
                          P.op("act", "activation", Cbf[h][:], C32[h][:], AF.Copy, reads=[C32b[h]], writes=[Cbfb[h]])
                      s_ = sm[g2]
                      den = HR[g2][:, :, 256]
                      P.op("dve", "scalar_tensor_tensor", s_[:, 0:4], den, -1.0, den, op0=ALU.mult, op1=ALU.max, reads=[HRb[g2]], writes=[smb[g2]])
                      P.op("dve", "tensor_tensor", s_[:, 4:8], s_[:, 0:4], EE[g6][:, 24 + d * 4:28 + d * 4], op=ALU.max,
                           reads=[smb[g2], EEb[g6]], writes=[smb[g2]])
                      P.op("dve", "reciprocal", s_[:, 8:12], s_[:, 4:8], reads=[smb[g2]], writes=[smb[g2]])
                      for h in range(4):
                          if d == 1:
                              P.op("dve", "tensor_scalar", HS[g2][:, h * 256:(h + 1) * 256], HR[g2][:, h, 0:256], s_[:, 8 + h:9 + h], None, op0=ALU.mult,
                                   reads=[HRb[g2], smb[g2]], writes=[HSb[g2]])
                          else:
                              P.op("dve", "scalar_tensor_tensor", HS[g2][:, h * 256:(h + 1) * 256], HR[g2][:, h, 0:256], s_[:, 8 + h:9 + h],
                                   HBl[g2][:, h * 256:(h + 1) * 256], op0=ALU.mult, op1=ALU.add,
                                   reads=[HRb[g2], smb[g2], HBlb[g2]], writes=[HSb[g2]])
                      if d == 1:
                          P.dma("pool", HB[t0:t0 + 128, :], HS[g2][:], reads=[HSb[g2]], writes=[sb_hb[ch]])
                      else:
                          for h in range(4):
                              P.op("dve", "bn_stats", bst[:, h, :], HS[g2][:, h * 256:(h + 1) * 256], reads=[HSb[g2]], writes=[bstb])
                              P.op("dve", "bn_aggr", bmv[:, h, :], bst[:, h, :], reads=[bstb], writes=[bstb])
                          P.op("act", "activation", brs[:], bmv[:, :, 1], AF.Sqrt, bias=LN_EPS, scale=1.0, reads=[bstb], writes=[bstb])
                          P.op("dve", "reciprocal", brs[:], brs[:], reads=[bstb], writes=[bstb])
                          for h in range(4):
                              P.op("dve", "tensor_scalar", YN[:, h * 256:(h + 1) * 256], HS[g2][:, h * 256:(h + 1) * 256],
                                   bmv[:, h, 0:1], brs[:, h:h + 1], op0=ALU.subtract, op1=ALU.mult,
                                   reads=[HSb[g2], bstb], writes=[YNb])
                          P.op("dve", "tensor_tensor", GO[:], mlg[:], Om[g6][:], op=ALU.mult, reads=[bc, Omb[g6]], writes=[GOb])
                          P.op("dve", "tensor_tensor", YM[:], YN[:], GO[:], op=ALU.mult, reads=[YNb, GOb], writes=[YMb])
                          for fc in range(8):
                              P.op("pe", "transpose", ymp[:, fc, :], YM[:, fc * 128:(fc + 1) * 128], ident_bf[:], reads=[YMb, bc], writes=[ympb])
                          P.op("act", "activation", YMTs[k2][:, :, cl * 128:(cl + 1) * 128], ymp[:, :, :], AF.Copy, reads=[ympb], writes=[YMTsb[k2]])
                          if it["last_sc"]:
                              P.dma("pool", fm_view(YMT)[:, :, sc * SC:(sc + 1) * SC], YMTs[k2][:], reads=[YMTsb[k2]], writes=[sb_ymt])
                      if nxt is not None and nxt["d"] == 0:
                          n2 = nxt["g2"]
                          P.dma("sp", HBl[n2][:], HB[nxt["ch"] * 128:(nxt["ch"] + 1) * 128, :], reads=[sb_hb[nxt["ch"]]], writes=[HBlb[n2]])

                  NI = len(items)
                  for i in range(-4, NI):
                      for off, fn in ((4, G1), (3, G2), (2, G3), (1, G4)):
                          if 0 <= i + off < NI:
                              fn(items[i + off])
                      if i >= 0:
                          S2(items[i], items[i + 1] if i + 1 < NI else None)
                  P.barrier()
              except _Stop:
                P.final_wait(P.allbufs)
                P.emit()
                return nc
            bc = P.buf()
        if stop_after == "ML":
            outbufs += [sb_ymt] + sb_hb

        if stop_after == "C":
            with contextlib.ExitStack() as pes:
                c = make_tile_ctx(pes)
                B, Bb, U, Ub, R32, Rb = c["B"], c["Bb"], c["U"], c["Ub"], c["R32"], c["Rb"]
                ost = [P.sb(f"ost{i}", [128, D], F32, pes) for i in range(2)]
                ostb = P.bufs(2)
                m1 = [P.sb(f"m1_{i}", [128, TT], F32, pes) for i in range(2)]
                m2 = [P.sb(f"m2_{i}", [128, TT], F32, pes) for i in range(2)]
                m1b, m2b = P.bufs(2), P.bufs(2)
                units = []
                for t in range(NTILE):
                    for hf in range(2):
                        units.append((wb["wpa"][hf * 1024:(hf + 1) * 1024, :].rearrange("(n p) c -> p n c", p=128), 8, 1024, wbuf["wpa"]))
                        units.append((wb["wpm"][hf * 1024:(hf + 1) * 1024, :].rearrange("(n p) c -> p n c", p=128), 8, 1024, wbuf["wpm"]))
                    for q in range(4):
                        units.append((wb["wout"][q * 512:(q + 1) * 512, :].rearrange("(n p) c -> p n c", p=128), 4, 2048, [wbuf["wout"][q]]))
                    units += ffn_units("wgu_b", "wd_b")
                UPT = 8 + 38
                ws = WStream(P, c["ring"], c["ringb"], units)
                ybufs = []
                for t in range(NTILE):
                    tok0 = t * TT
                    P.dma("sp", U[:, 0:16, :], fm_view(GA)[:, :, tok0:tok0 + TT], reads=sb_fm, writes=Ub[0:16])
                    P.dma("sp", U[:, 16:32, :], fm_view(GM)[:, :, tok0:tok0 + TT], reads=sb_fm, writes=Ub[16:32])
                    P.dma("sp", U[:, 48:56, :], fm_view(YAT)[:, :, tok0:tok0 + TT], reads=[sb_yat], writes=Ub[48:56])
                    P.dma("sp", U[:, 56:64, :], fm_view(YMT)[:, :, tok0:tok0 + TT], reads=[sb_ymt], writes=Ub[56:64])
                    P.dma("sp", R32[:], fm_view(XA1)[:, :, tok0:tok0 + TT], reads=[sb_xa1[t]], writes=sum(c["R32b"], []))
                    for f in range(KC):
                        hf, fl = f // 8, f % 8
                        Wa, Wab = ws.get(t * UPT + 2 * hf, depth=1)
                        Wm, Wmb = ws.get(t * UPT + 2 * hf + 1, depth=1)
                        ba, bm_ = f % 2, 2 + f % 2
                        for kc in range(8):
                            col = (fl * 8 + kc) * 128
                            P.op("pe", "matmul", B[ba][:], Wa[:, col:col + 128], U[:, 48 + kc, :], start=(kc == 0), stop=(kc == 7),
                                 reads=[Wab, Ub[48 + kc]], writes=[Bb[ba]])
                        for kc in range(8):
                            col = (fl * 8 + kc) * 128
                            P.op("pe", "matmul", B[bm_][:], Wm[:, col:col + 128], U[:, 56 + kc, :], start=(kc == 0), stop=(kc == 7),
                                 reads=[Wmb, Ub[56 + kc]], writes=[Bb[bm_]])
                        k = f % 2
                        P.op("dve", "tensor_tensor", m1[k][:], B[ba][:], U[:, f, :], op=ALU.mult, reads=[Bb[ba], Ub[f]], writes=[m1b[k]])
                        P.op("dve", "tensor_tensor", m2[k][:], B[bm_][:], U[:, 16 + f, :], op=ALU.mult, reads=[Bb[bm_], Ub[16 + f]], writes=[m2b[k]])
                        P.op("pool", "tensor_tensor", U[:, 32 + f, :], m1[k][:], m2[k][:], op=ALU.add, reads=[m1b[k], m2b[k]], writes=[Ub[32 + f]])
                    for f in range(KC):
                        q, fl = f // 4, f % 4
                        W, Wb = ws.get(t * UPT + 4 + q)
                        bk = 4 + f % 2
                        for kc in range(KC):
                            col = (fl * KC + kc) * 128
                            P.op("pe", "matmul", B[bk][:], W[:, col:col + 128], U[:, 32 + kc, :], start=(kc == 0), stop=(kc == KC - 1),
                                 reads=[Wb, Ub[32 + kc]], writes=[Bb[bk]])
                        if f > 0:
                            ln_stats_mm(c, f - 1)
                        P.op("dve", "tensor_tensor", R32[:, f, :], B[bk][:], R32[:, f, :], op=ALU.add,
                             reads=[Bb[bk]] + c["R32b"][f], writes=c["R32b"][f])
                        ln_stats_chunk(c, f)
                    ln_stats_mm(c, KC - 1)
                    ln_finalize(c, 2, ALPHA)
                    ffn_core(c, ws, t * UPT + 8)
                    ln_finalize(c, 4, 1.0)
                    for tb in range(4):
                        k2 = tb % 2
                        for b in range(4):
                            bank3 = B[b][:, :].rearrange("p (k t) -> p k t", k=4)
                            for k in range(4):
                                P.op("pe", "transpose", bank3[:, k, :], R32[:, 4 * b + k, tb * 128:(tb + 1) * 128], ident32[:],
                                     reads=c["R32b"][4 * b + k] + [bc], writes=[Bb[b]])
                            if b % 2 == 0:
                                P.op("act", "activation", ost[k2][:, b * 512:(b + 1) * 512], B[b][:], AF.Copy, reads=[Bb[b]], writes=[ostb[k2]])
                            else:
                                P.op("dve", "tensor_copy", ost[k2][:, b * 512:(b + 1) * 512], B[b][:], reads=[Bb[b]], writes=[ostb[k2]])
                        yb = P.buf()
                        P.dma("pool", y[tok0 + tb * 128: tok0 + (tb + 1) * 128, :], ost[k2][:], reads=[ostb[k2]], writes=[yb])
                        ybufs.append(yb)
                outbufs += ybufs
        P.final_wait(outbufs)
        P.emit()
    return nc


def _na_tables():
    NCB, QB, KB, KW = 4, 16, 32, 16
    q = np.arange(GW)
    ws = np.clip(q - KW // 2, 0, GW - KW)
    kc = np.arange(GW)
    valid = (kc[None, :] >= ws[:, None]) & (kc[None, :] < ws[:, None] + KW)
    col_idx = np.clip(kc[None, :] - q[:, None] + KW - 1, 0, 2 * KW - 2)
    return valid, col_idx


def prep_weights(inp):
    f32 = np.float32
    out = {}

    def gu(w):
        w = np.asarray(w[0], f32).reshape(KC, 128, 2, JC, 128)
        return np.ascontiguousarray(w.transpose(3, 1, 0, 2, 4)).reshape(JC * 128, 4096)

    def dn(w):
        w = np.asarray(w[0], f32).reshape(JC, 128, KC, 128)
        return np.ascontiguousarray(w.transpose(2, 1, 0, 3)).reshape(KC * 128, DFF)

    def fm(w, nk):
        ncols = w.shape[1]
        w = np.asarray(w, f32).reshape(nk, 128, ncols // 128, 128)
        return np.ascontiguousarray(w.transpose(2, 1, 0, 3)).reshape((ncols // 128) * 128, nk * 128)

    out["wgu_a"] = gu(inp["ffa_w_gu"]); out["wd_a"] = dn(inp["ffa_w_down"])
    out["wgu_b"] = gu(inp["ffb_w_gu"]); out["wd_b"] = dn(inp["ffb_w_down"])
    w_in = np.asarray(inp["mix_w_in"][0], f32)
    q_a, k_a, v_a, qk_m, v_m, o_m, gates, g_a, g_m = np.split(w_in, np.cumsum([1024, 1024, 1024, 2048, 1024, 1024, 16, 2048])[:8].tolist(), axis=1)
    out["win_fm"] = fm(np.concatenate([q_a, k_a, qk_m, g_a, g_m], axis=1), KC)
    tm = np.concatenate([v_a, v_m, o_m], axis=1).reshape(KC, 128, 6, 512)
    out["win_tm"] = np.ascontiguousarray(tm.transpose(2, 1, 0, 3)).reshape(6 * 128, 8192)
    out["win_gt"] = np.ascontiguousarray(gates.reshape(KC, 128, 16).transpose(1, 0, 2)).reshape(128, 256)
    out["wpa"] = fm(np.asarray(inp["mix_w_pa"][0], f32), 8)
    out["wpm"] = fm(np.asarray(inp["mix_w_pm"][0], f32), 8)
    out["wout"] = fm(np.asarray(inp["mix_w_out"][0], f32), KC)
    vecs = [inp[k][0] for k in ("norm_a_g", "norm_a_b", "norm_m_g", "norm_m_b", "norm_b_g", "norm_b_b")]
    out["nrm"] = np.ascontiguousarray(np.concatenate([np.asarray(v, f32).reshape(KC, 128).T for v in vecs], axis=1))
    cw = np.asarray(inp["ml_conv_w"][0], f32)
    out["convw"] = np.ascontiguousarray(cw.reshape(5, KC, 128).transpose(2, 1, 0)).reshape(128, 80)
    out["convb"] = np.ascontiguousarray(np.asarray(inp["ml_conv_b"][0], f32).reshape(KC, 128).T)
    out["convbr"] = np.ascontiguousarray(np.asarray(inp["ml_conv_b"][0], f32).reshape(1, 2048))
    out["gateb"] = np.ascontiguousarray(np.broadcast_to(np.asarray(inp["ml_gate_b"][0], f32)[None, :], (128, 16)))
    out["mlg"] = np.ascontiguousarray(np.broadcast_to(np.asarray(inp["ml_norm_g"][0], f32)[None, :], (128, 1024)))
    valid, col_idx = _na_tables()
    rpb = np.asarray(inp["na_rpb"][0], f32)
    g = rpb[:, :, col_idx]
    out["rpbg"] = np.ascontiguousarray(g.transpose(0, 2, 1, 3)).reshape(512, 960)
    cm = np.where(valid, 0.0, NEG).astype(f32)
    out["cmask"] = np.ascontiguousarray(np.broadcast_to(cm[:, None, :], (64, 15, 64))).reshape(64, 960)
    return out


_NC_CACHE = {}


def kernel(**inputs):
    xp = np.asarray(inputs["x_prompt"], np.float32)
    xs = np.asarray(inputs["x_sample"], np.float32)
    w = prep_weights(inputs)
    core_x = [xp[0]] + [np.ascontiguousarray(xs[2 * i:2 * i + 2].reshape(NT_FULL, D)) for i in range(4)]
    core_x += [core_x[4]] * 3
    in_maps = []
    for ci in range(N_CORES):
        m = dict(w)
        m["x"] = core_x[ci]
        m["cont"] = np.full((128, 1), 1.0 if ci == 0 else 0.0, np.float32)
        in_maps.append(m)
    if "nc" not in _NC_CACHE:
        _NC_CACHE["nc"] = build_program(NT_FULL)
    res = run_bass_kernel_spmd(_NC_CACHE["nc"], in_maps, core_ids=list(range(N_CORES)))
    ys = [np.asarray(r["y"], np.float32) for r in res.results]
    y_prompt = ys[0].reshape(1, NT_FULL, D)
    y_sample = np.concatenate([ys[1 + i].reshape(2, NT_FULL // 2, D) for i in range(4)], axis=0)
    return (y_prompt, y_sample)
```
